# Optimizing a Trainium2 kernel written in Bass

```python
import jax, jax.numpy as jnp
from jax import lax
import numpy as np

D_MODEL = 1024
BATCH = 8
SEQ = 2048
DEPTH = 2

BRANCH_W = D_MODEL // 2
N_BRANCH = 3
N_FOURIER_GROUPS = 4
FOURIER_GW = BRANCH_W // N_FOURIER_GROUPS
CONF_K = 31
SHORT_K = 3
EPS = 1e-6

SPLIT_SIZES = (
    BRANCH_W, BRANCH_W,
    BRANCH_W, BRANCH_W, BRANCH_W,
    BRANCH_W, BRANCH_W, BRANCH_W, BRANCH_W,
    N_BRANCH * D_MODEL,
)
IN_W = sum(SPLIT_SIZES)
SPLIT_IDX = tuple(int(v) for v in np.cumsum(SPLIT_SIZES)[:-1])

kernel_name = "hybrid_fourier_conformer_shortconv_encoder"


def rms_norm(x, g):
    xf = x.astype(jnp.float32)
    y = xf * lax.rsqrt(jnp.mean(xf * xf, axis=-1, keepdims=True) + EPS)
    return (y * g.astype(jnp.float32)).astype(x.dtype)


def layer_norm(x, g, b):
    xf = x.astype(jnp.float32)
    mu = jnp.mean(xf, axis=-1, keepdims=True)
    var = jnp.mean(jnp.square(xf - mu), axis=-1, keepdims=True)
    y = (xf - mu) * lax.rsqrt(var + EPS)
    return (y * g.astype(jnp.float32) + b.astype(jnp.float32)).astype(x.dtype)


def depthwise_conv(u, w, b):
    k, c = w.shape
    pad = (k - 1) // 2
    out = lax.conv_general_dilated(
        u, w[:, None, :].astype(u.dtype), window_strides=(1,), padding=[(pad, pad)],
        dimension_numbers=("NWC", "WIO", "NWC"), feature_group_count=c)
    return out + b


def fourier_mix(u):
    bsz, s, w = u.shape
    ug = u.reshape(bsz, s, N_FOURIER_GROUPS, FOURIER_GW).astype(jnp.float32)
    f = jnp.fft.fftn(ug, axes=(1, 3), norm="ortho").real
    return f.reshape(bsz, s, w).astype(u.dtype)


def setup_inputs(seed: int = 0) -> dict:
    key = jax.random.key(seed)
    ks = jax.random.split(key, 16)
    f32 = jnp.float32
    x = jax.random.normal(ks[0], (BATCH, SEQ, D_MODEL), f32)
    norm_g = 1.0 + 0.05 * jax.random.normal(ks[1], (DEPTH, D_MODEL), f32)
    w_in = jax.random.normal(ks[2], (DEPTH, D_MODEL, IN_W), f32) * D_MODEL ** -0.5
    b_in = 0.01 * jax.random.normal(ks[3], (DEPTH, IN_W), f32)
    conv_c_w = jax.random.normal(ks[4], (DEPTH, CONF_K, BRANCH_W), f32) * CONF_K ** -0.5
    conv_c_b = 0.01 * jax.random.normal(ks[5], (DEPTH, BRANCH_W), f32)
    ln_c_g = 1.0 + 0.05 * jax.random.normal(ks[6], (DEPTH, BRANCH_W), f32)
    ln_c_b = 0.01 * jax.random.normal(ks[7], (DEPTH, BRANCH_W), f32)
    conv_s_w = jax.random.normal(ks[8], (DEPTH, SHORT_K, BRANCH_W), f32) * SHORT_K ** -0.5
    conv_s_b = 0.01 * jax.random.normal(ks[9], (DEPTH, BRANCH_W), f32)
    w_branch = jax.random.normal(ks[10], (DEPTH, N_BRANCH, BRANCH_W, D_MODEL), f32) * BRANCH_W ** -0.5
    w_out = jax.random.normal(ks[11], (DEPTH, D_MODEL, D_MODEL), f32) * D_MODEL ** -0.5
    final_g = 1.0 + 0.05 * jax.random.normal(ks[12], (D_MODEL,), f32)
    return {"x": x, "norm_g": norm_g, "w_in": w_in, "b_in": b_in,
            "conv_c_w": conv_c_w, "conv_c_b": conv_c_b, "ln_c_g": ln_c_g, "ln_c_b": ln_c_b,
            "conv_s_w": conv_s_w, "conv_s_b": conv_s_b, "w_branch": w_branch,
            "w_out": w_out, "final_g": final_g}


def hybrid_layer(x, norm_g, w_in, b_in, conv_c_w, conv_c_b, ln_c_g, ln_c_b,
                 conv_s_w, conv_s_b, w_branch, w_out):
    bsz, s, d = x.shape
    h = rms_norm(x, norm_g)
    p = jnp.einsum("bsd,de->bse", h, w_in) + b_in
    (f_x, f_z, c_a, c_b, c_z, s_bg, s_cg, s_h, s_z, gates) = jnp.split(p, SPLIT_IDX, axis=-1)

    y_f = fourier_mix(f_x) * jax.nn.silu(f_z)

    c = c_a * jax.nn.sigmoid(c_b)
    c = depthwise_conv(c, conv_c_w, conv_c_b)
    c = jax.nn.silu(layer_norm(c, ln_c_g, ln_c_b))
    y_c = c * jax.nn.silu(c_z)

    sc = s_bg * depthwise_conv(s_cg * s_h, conv_s_w, conv_s_b)
    y_s = sc * jax.nn.silu(s_z)

    y = jnp.stack([y_f, y_c, y_s], axis=2)
    yb = jnp.einsum("bskw,kwd->bskd", y, w_branch)
    g = jax.nn.sigmoid(gates.reshape(bsz, s, N_BRANCH, d))
    m = jnp.sum(g * yb, axis=2)
    return x + jnp.einsum("bsd,de->bse", m, w_out)


def reference(x, norm_g, w_in, b_in, conv_c_w, conv_c_b, ln_c_g, ln_c_b,
              conv_s_w, conv_s_b, w_branch, w_out, final_g):
    for l in range(DEPTH):
        x = hybrid_layer(x, norm_g[l], w_in[l], b_in[l], conv_c_w[l], conv_c_b[l],
                         ln_c_g[l], ln_c_b[l], conv_s_w[l], conv_s_b[l],
                         w_branch[l], w_out[l])
    return rms_norm(x, final_g)
```

```python
import bisect
from contextlib import ExitStack

import numpy as np
import ml_dtypes

import concourse.bass as bass
import concourse.mybir as mybir
from concourse.bass_utils import run_bass_kernel_spmd

F32 = mybir.dt.float32
BF16 = mybir.dt.bfloat16
U8 = mybir.dt.uint8
AF = mybir.ActivationFunctionType
ALU = mybir.AluOpType

P = 128
D = 1024
S = 2048
KD = 8
TW = 512
NT = 4
BW = 512
NCH = 4
IN_W = 7680
CONF_K = 31
DEPTH = 2
EPS = 1e-6
NSLOT = 4
SLOT_ELEMS = 2048
LC = 220
DSIZE = {F32: 4, BF16: 2, U8: 1}


class Sem:
    def __init__(self, h, name):
        self.h = h
        self.name = name
        self.count = 0


class Region:
    def __init__(self, size):
        self.los = [0]
        self.ivs = [[0, size, None, {}]]

    def _split(self, pos):
        i = bisect.bisect_right(self.los, pos) - 1
        iv = self.ivs[i]
        if iv[0] == pos or pos >= iv[1]:
            return
        new = [pos, iv[1], iv[2], dict(iv[3])]
        iv[1] = pos
        self.los.insert(i + 1, pos)
        self.ivs.insert(i + 1, new)

    def get(self, lo, hi):
        self._split(lo)
        self._split(hi)
        i = bisect.bisect_left(self.los, lo)
        out = []
        while i < len(self.ivs) and self.ivs[i][0] < hi:
            out.append(self.ivs[i])
            i += 1
        return out


class V:
    def __init__(self, ap, ranges):
        self.ap = ap
        self.ranges = ranges


class Buf:
    def __init__(self, region_name, base_ap, dtype, shape, byte_off):
        self.rn = region_name
        self.ap = base_ap
        self.dtype = dtype
        self.shape = tuple(shape)
        self.off = byte_off
        self.es = DSIZE[dtype]

    def __getitem__(self, idx):
        if not isinstance(idx, tuple):
            idx = (idx,)
        idx = tuple(idx) + (slice(None),) * (len(self.shape) - len(idx))
        ap = self.ap[(slice(None),) + idx]
        strides = []
        s = 1
        for n in reversed(self.shape):
            strides.append(s)
            s *= n
        strides = list(reversed(strides))
        ranges = [(0, 1)]
        offs = [0]
        for d, (ix, n) in enumerate(zip(idx, self.shape)):
            st = strides[d]
            if isinstance(ix, int):
                lo, hi = ix, ix + 1
            else:
                lo = 0 if ix.start is None else ix.start
                hi = n if ix.stop is None else ix.stop
            last = d == len(self.shape) - 1
            if last:
                offs = [(o + lo * st, (hi - lo) * st) for o in offs]
            else:
                offs = [o + i * st for o in offs for i in range(lo, hi)]
        rr = []
        for o, ln in offs:
            a = self.off + o * self.es
            b = a + ln * self.es
            if rr and rr[-1][2] == a and rr[-1][0] == self.rn:
                rr[-1] = (self.rn, rr[-1][1], b)
            else:
                rr.append((self.rn, a, b))
        return V(ap, rr)

    def all(self):
        return self[tuple(slice(None) for _ in self.shape)]


class Eng:
    def __init__(self, name, e, sem, self_sync=True):
        self.name = name
        self.e = e
        self.sem = sem
        self.seen = {}
        self.self_sync = self_sync
        self.ninst = 0

    def wait(self, ev):
        sem, val = ev
        if sem is self.sem and not self.self_sync:
            return
        if self.seen.get(sem, 0) >= val:
            return
        self.e.wait_ge(sem.h, val)
        self.seen[sem] = val


class Prog:
    def __init__(self, nc, es):
        self.nc = nc
        self.es = es
        self.regions = {}
        self.nsem = 0

    def new_sem(self, name):
        h = self.es.enter_context(self.nc.semaphore(name))
        return Sem(h, name)

    def sbuf(self, name, nbytes):
        t = self.es.enter_context(self.nc.sbuf_tensor(name, [P, nbytes], U8))
        self.regions[name] = Region(nbytes)
        return t

    def view(self, t, name, byte_off, dtype, shape):
        n = int(np.prod(shape))
        ap = t[:, byte_off:byte_off + n * DSIZE[dtype]]
        if dtype != U8:
            ap = ap.bitcast(dtype)
        if len(shape) == 2:
            ap = ap.rearrange("p (a b) -> p a b", a=shape[0])
        elif len(shape) == 3:
            ap = ap.rearrange("p (a b c) -> p a b c", a=shape[0], b=shape[1])
        elif len(shape) == 4:
            ap = ap.rearrange("p (a b c d) -> p a b c d", a=shape[0], b=shape[1], c=shape[2])
        return Buf(name, ap, dtype, shape, byte_off)

    def psum(self, name):
        t = self.es.enter_context(self.nc.psum_tensor(name, [P, TW], F32))
        self.regions[name] = Region(TW * 4)
        return Buf(name, t[:, :], F32, (TW,), 0)

    def _deps(self, reads, writes):
        deps = {}

        def add(ev):
            if ev is None:
                return
            s, v = ev
            if deps.get(s, 0) < v:
                deps[s] = v

        for v in reads:
            for rn, lo, hi in v.ranges:
                for iv in self.regions[rn].get(lo, hi):
                    add(iv[2])
        for v in writes:
            for rn, lo, hi in v.ranges:
                for iv in self.regions[rn].get(lo, hi):
                    add(iv[2])
                    for s, val in iv[3].items():
                        add((s, val))
        return deps

    def _commit(self, reads, writes, ev):
        s, val = ev
        for v in reads:
            for rn, lo, hi in v.ranges:
                for iv in self.regions[rn].get(lo, hi):
                    if iv[3].get(s, 0) < val:
                        iv[3][s] = val
        for v in writes:
            for rn, lo, hi in v.ranges:
                for iv in self.regions[rn].get(lo, hi):
                    iv[2] = ev
                    iv[3] = {}

    def op(self, eng, build, reads=(), writes=()):
        deps = self._deps(reads, writes)
        for s, v in deps.items():
            eng.wait((s, v))
        inst = build()
        eng.sem.count += 1
        inst.then_inc(eng.sem.h, 1)
        eng.ninst += 1
        ev = (eng.sem, eng.sem.count)
        self._commit(reads, writes, ev)
        return ev

    def mm_group(self, eng, mms, reads, writes):
        deps = self._deps(reads, writes)
        for s, v in deps.items():
            eng.wait((s, v))
        inst = None
        for (o, l, r, st, sp) in mms:
            inst = self.nc.tensor.matmul(o, lhsT=l, rhs=r, start=st, stop=sp)
            eng.ninst += 1
        eng.sem.count += 1
        inst.then_inc(eng.sem.h, 1)
        ev = (eng.sem, eng.sem.count)
        self._commit(reads, writes, ev)
        return ev

    def dma(self, eng, sem, out_ap, in_ap, reads=(), writes=()):
        deps = self._deps(reads, writes)
        for s, v in deps.items():
            eng.wait((s, v))
        inst = eng.e.dma_start(out=out_ap, in_=in_ap)
        sem.count += 16
        inst.then_inc(sem.h, 16)
        ev = (sem, sem.count)
        self._commit(reads, writes, ev)
        return ev


def build_program(depth=DEPTH, n_witems=None, final_norm=True):
    nc = bass.Bass("TRN2", target_bir_lowering=False)
    w_items = []

    xin = nc.dram_tensor("xT", [NT, P, KD * TW], F32, kind="ExternalInput").ap()
    n_w = n_witems if n_witems is not None else 40 * depth
    ws = nc.dram_tensor("ws", [n_w, P, SLOT_ELEMS], F32, kind="ExternalInput").ap()
    dft = nc.dram_tensor("dft", [8, P, SLOT_ELEMS], BF16, kind="ExternalInput").ap()
    cst_d = nc.dram_tensor("cst", [P, DEPTH * LC + 10], F32, kind="ExternalInput").ap()
    ctab_d = nc.dram_tensor("ctab", [P, 512], BF16, kind="ExternalInput").ap()
    yout = nc.dram_tensor("yT", [NT, P, KD * TW], F32, kind="ExternalOutput").ap()

    with ExitStack() as es:
        pg = Prog(nc, es)
        t_x = pg.sbuf("xT_r", KD * S * 4)
        t_h = pg.sbuf("hT_r", KD * S * 2)
        t_m = pg.sbuf("M_r", KD * S * 2)
        t_y = pg.sbuf("Y_r", NCH * S * 2)
        t_ring = pg.sbuf("ring_r", NSLOT * SLOT_ELEMS * 2)
        TEMP_BYTES = 20 * 1024
        t_tmp = pg.sbuf("tmp_r", TEMP_BYTES)
        t_cf = pg.sbuf("CF_r", NCH * S * 2 + S * 4)
        NCST = DEPTH * LC + 10
        t_cst = pg.sbuf("cst_r", NCST * 4)
        t_ctab = pg.sbuf("ctab_r", 512 * 2 + 2 * 128 * 2 + 2 * 128 * 2)

        xT = pg.view(t_x, "xT_r", 0, F32, (KD, S))
        hT = pg.view(t_h, "hT_r", 0, BF16, (KD, S))
        Mb = pg.view(t_m, "M_r", 0, BF16, (KD, S))
        CT = pg.view(t_cf, "CF_r", 0, BF16, (NCH, S))
        ACCF = pg.view(t_cf, "CF_r", NCH * S * 2, F32, (S,))
        Y = pg.view(t_y, "Y_r", 0, BF16, (NCH, S))
        slots = [pg.view(t_ring, "ring_r", i * SLOT_ELEMS * 2, BF16, (SLOT_ELEMS,)) for i in range(NSLOT)]
        cst = pg.view(t_cst, "cst_r", 0, F32, (NCST,))
        ctab = pg.view(t_ctab, "ctab_r", 0, BF16, (512,))
        Ez = pg.view(t_ctab, "ctab_r", 1024, BF16, (128,))
        Ez2 = pg.view(t_ctab, "ctab_r", 1280, BF16, (128,))
        DG2 = pg.view(t_ctab, "ctab_r", 1536, BF16, (2, 128))
        YQ = pg.view(t_y, "Y_r", 0, BF16, (NCH, S // 2, 2))

        banks = [pg.psum(f"ps{i}") for i in range(8)]
        bank_i = [0]

        held = set()

        def nextbank():
            while True:
                b = banks[bank_i[0] % 8]
                bank_i[0] += 1
                if id(b) not in held:
                    return b

        def hold(b):
            held.add(id(b))

        def release(b):
            held.discard(id(b))

        PE = Eng("pe", nc.tensor, pg.new_sem("s_pe"), self_sync=False)
        ACT = Eng("act", nc.scalar, pg.new_sem("s_act"))
        DVE = Eng("dve", nc.vector, pg.new_sem("s_dve"))
        POOL = Eng("pool", nc.gpsimd, pg.new_sem("s_pool"))
        SP = Eng("sp", nc.sync, pg.new_sem("s_sp"))
        slot_sems = [pg.new_sem(f"s_slot{i}") for i in range(NSLOT)]
        slot_sems_d = [pg.new_sem(f"s_dslot{i}") for i in range(NSLOT)]
        s_in = [pg.new_sem(f"s_in{i}") for i in range(NT)]
        s_out = pg.new_sem("s_out")
        s_c = [pg.new_sem("s_c0"), pg.new_sem("s_c1")]
        ring_i = [0]

        def cc(col):
            return cst[col:col + 1]

        def tv(off, dtype, shape):
            n = int(np.prod(shape)) * DSIZE[dtype]
            assert off + n <= TEMP_BYTES, (off, n)
            return pg.view(t_tmp, "tmp_r", off, dtype, shape)

        live = [False] * NSLOT
        ev_x0_box = [None] * NT

        def pick_slot():
            best, bestv = None, None
            for i in range(NSLOT):
                if live[i]:
                    continue
                v = 0
                for iv in pg.regions["ring_r"].get(i * SLOT_ELEMS * 2, (i + 1) * SLOT_ELEMS * 2):
                    v = max(v, iv[3].get(PE.sem, 0))
                if best is None or v < bestv:
                    best, bestv = i, v
            assert best is not None, "ring: no free slot"
            live[best] = True
            return best

        def done(sl):
            live[slots.index(sl)] = False

        def load_w(desc):
            if len(w_items) == 1:
                POOL.wait(ev_x0_box[2])
            i = pick_slot()
            sl = slots[i]
            idx = len(w_items)
            w_items.append(desc)
            pg.dma(POOL, slot_sems[i], sl.ap, ws[idx], writes=[sl.all()])
            return sl

        def load_dft(idx):
            i = pick_slot()
            sl = slots[i]
            pg.dma(SP, slot_sems_d[i], sl.ap, dft[idx], writes=[sl.all()])
            return sl

        def act(out_v, in_v, func, bias=None, scale=None, extra_reads=()):
            kw = {}
            if bias is not None:
                kw["bias"] = bias.ap
            if scale is not None:
                kw["scale"] = scale.ap if isinstance(scale, V) else scale
            rd = [in_v] + list(extra_reads)
            if bias is not None:
                rd.append(bias)
            if isinstance(scale, V):
                rd.append(scale)
            return pg.op(ACT, lambda: nc.scalar.activation(out=out_v.ap, in_=in_v.ap, func=func, **kw),
                         reads=rd, writes=[out_v])

        def tt(eng, out_v, a_v, b_v, op):
            return pg.op(eng, lambda: eng.e.tensor_tensor(out=out_v.ap, in0=a_v.ap, in1=b_v.ap, op=op),
                         reads=[a_v, b_v], writes=[out_v])

        def stt(eng, out_v, a_v, sc_v, b_v, op0, op1):
            return pg.op(eng, lambda: eng.e.scalar_tensor_tensor(out=out_v.ap, in0=a_v.ap, scalar=sc_v.ap,
                                                                 in1=b_v.ap, op0=op0, op1=op1),
                         reads=[a_v, sc_v, b_v], writes=[out_v])

        def ts2(eng, out_v, a_v, s1_v, s2_v, op0, op1):
            return pg.op(eng, lambda: eng.e.tensor_scalar(out=out_v.ap, in0=a_v.ap, scalar1=s1_v.ap,
                                                          scalar2=s2_v.ap, op0=op0, op1=op1),
                         reads=[a_v, s1_v, s2_v], writes=[out_v])

        def tsl(t):
            return slice(t * TW, (t + 1) * TW)

        def bg_tick(n):
            bg.tick(n)

        def inproj(slot, which, t, nk=KD, rhs=None, cols=None):
            rhs = hT if rhs is None else rhs
            b = nextbank()
            mms = []
            rd = []
            per = nk * 128
            csl = tsl(t) if cols is None else slice(cols[0], cols[1])
            ov = b.all() if cols is None else b[0:cols[1] - cols[0]]
            for k in range(nk):
                lo = which * per + k * 128
                lv = slot[lo:lo + 128]
                rv = rhs[k, csl]
                mms.append((ov.ap, lv.ap, rv.ap, k == 0, k == nk - 1))
                rd.append(rv)
            rd.append(slot[which * per:(which + 1) * per])
            pg.mm_group(PE, mms, reads=rd, writes=[ov])
            bg_tick(nk)
            return b

        xin_r = xin.rearrange("t p (k n) -> t p k n", k=KD)
        yout_r = yout.rearrange("t p (k n) -> t p k n", k=KD)
        ev_x0 = None
        for t in range(NT):
            v = xT[:, tsl(t)]
            ev = pg.dma(SP, s_in[t], v.ap, xin_r[t], writes=[v])
            ev_x0_box[t] = ev
            if t == 0:
                ev_x0 = ev
                pg.dma(SP, s_c[0], cst.ap, cst_d[:, :], writes=[cst.all()])
                pg.dma(SP, s_c[1], ctab.ap, ctab_d[:, :], writes=[ctab.all()])
            if t == 1:
                SP.wait(ev_x0)
        CC_T = ctab[0:128]
        MS_T = ctab[128:256]
        ONES = ctab[256:384]
        IDENT = ctab[384:512]
        pg.op(DVE, lambda: nc.vector.memset(Ez.ap, 0.0), writes=[Ez.all()])
        pg.op(DVE, lambda: nc.vector.memset(Ez2.ap, 0.0), writes=[Ez2.all()])
        EPSC = DEPTH * LC + 8
        ZEROC = DEPTH * LC + 9

        class BG:
            def __init__(self):
                self.q = []
                self.credit = 0.0
                self.quota = 13.0
                self.paused = False

            def add(self, th):
                self.q.append(th)

            def tick(self, n):
                if not self.q:
                    self.credit = 0.0
                    return
                self.credit += n
                if self.paused:
                    return
                while self.q and self.credit >= self.quota:
                    self.credit -= self.quota
                    self.q.pop(0)()

            def pause(self):
                self.paused = True

            def resume(self):
                self.paused = False
                self.tick(0)

            def drain(self):
                while self.q:
                    self.q.pop(0)()
                self.credit = 0.0

        bg = BG()
        KPE = [15, 14, 16]
        dg_i = [0]
        QUOTA = {"front": 12.0, "F": 15.0, "Fg": 13.0, "S": 15.0, "Sg": 15.0}

        def rstd_from(dst, src_v, scale=1.0):
            act(dst, src_v, AF.Ln, bias=cc(EPSC), scale=scale)
            act(dst, dst, AF.Exp, scale=-0.5)

        def rms_stages(gcol, to_hT, tiles=None):
            SQ = tv(0, BF16, (2, KD, TW))
            RS = tv(16384, F32, (2, TW))
            bks = {}
            tiles = [(t * TW, (t + 1) * TW) for t in range(NT)] if tiles is None else tiles
            n_t = len(tiles)

            def stage_a(i):
                lo, hi = tiles[i]
                w = hi - lo
                sq = SQ[i % 2, :, 0:w]
                pg.op(ACT, lambda: nc.scalar.activation(out=sq.ap, in_=xT[:, lo:hi].ap, func=AF.Square),
                      reads=[xT[:, lo:hi]], writes=[sq])
                b = nextbank()
                mms = [(b[0:w].ap, ONES.ap, SQ[i % 2, k, 0:w].ap, k == 0, k == KD - 1) for k in range(KD)]
                pg.mm_group(PE, mms, reads=[ONES, sq], writes=[b[0:w]])
                bks[i] = b
                hold(b)

            def stage_b(i):
                lo, hi = tiles[i]
                w = hi - lo
                rs = RS[i % 2, 0:w]
                rstd_from(rs, bks[i][0:w], scale=1.0 / D)
                release(bks[i])
                for k in range(KD):
                    if to_hT:
                        stt(DVE, hT[k, lo:hi], xT[k, lo:hi], cc(gcol + k), rs, ALU.mult, ALU.mult)
                    else:
                        stt(DVE, xT[k, lo:hi], xT[k, lo:hi], cc(gcol + k), rs, ALU.mult, ALU.mult)
                        if k % 2 == 1:
                            v = xT[k - 1:k + 1, lo:hi]
                            pg.dma(SP, s_out, yout_r[lo // TW, :, k - 1:k + 1, lo % TW:lo % TW + w], v.ap, reads=[v])

            return stage_a, stage_b, tiles

        def rmsnorm_first(gcol):
            stage_a, stage_b, tiles = rms_stages(gcol, True)
            n_t = len(tiles)
            stage_a(0)
            stage_b(0)
            stage_a(1)
            for i in range(1, n_t):
                if i + 1 < n_t:
                    stage_a(i + 1)
                stage_b(i)

        def gating_steps(l, br, first, ysrc):
            base = l * LC
            GT = tv(14336, F32, (2, TW))
            TM = tv(18432, BF16, (2, TW))
            st = {"n": 0}
            steps = []
            for half in range(2):
                for i in range(2):
                    for w in range(2):
                        for t in range(NT):
                            def step(half=half, i=i, w=w, t=t):
                                j0 = 36 + 8 * br + 4 * half + 2 * i
                                dc = 4 * half + 2 * i + w
                                if t == 0 and w == 0 and i == 0:
                                    st["b"] = load_w(("br", l, br, half))
                                if t == 0 and w == 0:
                                    st["g"] = load_w(("in", l, j0, j0 + 1))
                                last_g = (t == NT - 1 and w == 1)
                                last_b = last_g and i == 1
                                n = st["n"]
                                st["n"] += 1
                                bgk = inproj(st["g"], w, t)
                                gt = GT[n % 2]
                                act(gt, bgk.all(), AF.Sigmoid, bias=cc(base + j0 + w))
                                by = inproj(st["b"], dc % 4, t, nk=NCH, rhs=ysrc)
                                if first:
                                    tt(DVE, Mb[dc, tsl(t)], by.all(), gt, ALU.mult)
                                else:
                                    tm = TM[n % 2]
                                    tt(DVE, tm, by.all(), gt, ALU.mult)
                                    tt(DVE, Mb[dc, tsl(t)], tm, Mb[dc, tsl(t)], ALU.add)
                                if last_g:
                                    done(st["g"])
                                if last_b:
                                    done(st["b"])
                            steps.append(step)
            return steps

        def gating(l, br, first, ysrc):
            for stp in gating_steps(l, br, first, ysrc):
                stp()

        def conformer_front(l):
            base = l * LC
            CW = base + 68
            CB = base + 192
            SG = tv(0, F32, (2, TW))
            n = 0
            bg.quota = QUOTA["front"]
            for ch in range(NCH):
                slot = load_w(("in", l, 12 + ch, 8 + ch))
                for t in range(NT):
                    bb = inproj(slot, 0, t)
                    sg = SG[n % 2]
                    n += 1
                    act(sg, bb.all(), AF.Sigmoid, bias=cc(base + 12 + ch))
                    ba = inproj(slot, 1, t)
                    stt(DVE, CT[ch, tsl(t)], ba.all(), cc(base + 8 + ch), sg, ALU.add, ALU.mult)
                done(slot)
                st = {}

                def init(ch=ch):
                    ts2(DVE, ACCF.all(), CT[ch], cc(ZEROC), cc(CB + ch), ALU.mult, ALU.add)

                def tap(k, ch=ch):
                    d = k - 15
                    lo = max(0, -d)
                    hi = S - max(0, d)
                    stt(DVE, ACCF[lo:hi], CT[ch, lo + d:hi + d], cc(CW + ch * 31 + k), ACCF[lo:hi],
                        ALU.mult, ALU.add)

                def pe_part(ch=ch, st=st):
                    bks = [nextbank() for _ in range(NT)]
                    for b in bks:
                        hold(b)
                    st["bks"] = bks
                    for i, k in enumerate(KPE):
                        dg = DG2[dg_i[0] % 2]
                        dg_i[0] += 1
                        wk = cc(CW + ch * 31 + k)
                        pg.op(ACT, lambda: nc.scalar.activation(out=dg.ap, in_=IDENT.ap, func=AF.Copy, scale=wk.ap),
                              reads=[IDENT, wk], writes=[dg])
                        d = k - 15
                        mms, rd, wr = [], [dg], []
                        for t in range(NT):
                            clo = max(0, -(t * TW + d))
                            chi = min(TW, S - t * TW - d)
                            rv = CT[ch, t * TW + clo + d:t * TW + chi + d]
                            ov = bks[t][clo:chi]
                            rd.append(rv)
                            wr.append(ov)
                            mms.append((ov.ap, dg.ap, rv.ap, i == 0, i == len(KPE) - 1))
                        pg.mm_group(PE, mms, reads=rd, writes=wr)

                def merge(ch=ch, st=st):
                    for t in range(NT):
                        b = st["bks"][t]
                        tt(DVE, CT[ch, tsl(t)], b.all(), ACCF[tsl(t)], ALU.add)
                        release(b)

                dve_taps = [k for k in range(CONF_K) if k not in KPE]
                bg.add(init)
                for i, k in enumerate(dve_taps):
                    if i == len(dve_taps) - 6:
                        bg.add(pe_part)
                    bg.add(lambda k=k, tap=tap: tap(k))
                bg.add(merge)

        def conformer_back(l, fillers=()):
            base = l * LC
            LG = base + 196
            LB = base + 200
            VS = tv(0, BF16, (NCH, TW))
            MV = tv(4096, F32, (NT, TW))
            T1s = [tv(12288, F32, (TW,)), tv(0, F32, (TW,))]
            S1s = [tv(14336, F32, (TW,)), tv(2048, F32, (TW,))]
            S2 = tv(16384, F32, (2, TW))
            bms = []
            fillers = list(fillers)
            for t in range(NT):
                if fillers:
                    fillers.pop(0)()
                pg.op(ACT, lambda: nc.scalar.activation(out=VS.ap, in_=CT[:, tsl(t)].ap, func=AF.Square),
                      reads=[CT[:, tsl(t)]], writes=[VS.all()])
                bm = nextbank()
                pg.mm_group(PE, [(bm.ap, ONES.ap, CT[ch, tsl(t)].ap, ch == 0, ch == NCH - 1) for ch in range(NCH)],
                            reads=[ONES] + [CT[ch, tsl(t)] for ch in range(NCH)], writes=[bm.all()])
                hold(bm)
                bms.append(bm)
                be = nextbank()
                pg.mm_group(PE, [(be.ap, ONES.ap, VS[ch].ap, ch == 0, ch == NCH - 1) for ch in range(NCH)],
                            reads=[ONES, VS.all()], writes=[be.all()])
                act(MV[t], bm.all(), AF.Square, scale=1.0 / BW)
                mvt = MV[t]
                pg.op(DVE, lambda: nc.vector.scalar_tensor_tensor(out=mvt.ap, in0=be.ap, scalar=1.0 / BW, in1=mvt.ap,
                                                                  op0=ALU.mult, op1=ALU.subtract),
                      reads=[be.all(), mvt], writes=[mvt])
                pg.op(DVE, lambda: nc.vector.tensor_scalar_max(out=MV[t].ap, in0=MV[t].ap, scalar1=0.0),
                      reads=[MV[t]], writes=[MV[t]])
            while fillers:
                fillers.pop(0)()
            for t in range(NT):
                act(MV[t], MV[t], AF.Ln, bias=cc(EPSC), scale=1.0)
            for t in range(NT):
                act(MV[t], MV[t], AF.Exp, scale=-0.5)
            zslots = [load_w(("in", l, 16, 17)), load_w(("in", l, 18, 19))]
            iters = [(t, ch) for t in range(NT) for ch in range(NCH)]

            def part_a(i):
                t, ch = iters[i]
                bm = bms[t]
                s2, T1, S1 = S2[i % 2], T1s[i % 2], S1s[i % 2]
                bz = inproj(zslots[ch // 2], ch % 2, t)
                act(s2, bz.all(), AF.Silu, bias=cc(base + 16 + ch))
                ctv = CT[ch, tsl(t)]
                pg.op(DVE, lambda: nc.vector.scalar_tensor_tensor(out=T1.ap, in0=bm.ap, scalar=-1.0 / BW, in1=ctv.ap,
                                                                  op0=ALU.mult, op1=ALU.add),
                      reads=[bm.all(), ctv], writes=[T1.all()])
                tt(DVE, T1.all(), T1.all(), MV[t], ALU.mult)
                act(S1.all(), T1.all(), AF.Silu, bias=cc(LB + ch), scale=cc(LG + ch))

            def part_b(i):
                t, ch = iters[i]
                s2, S1 = S2[i % 2], S1s[i % 2]
                tt(DVE, CT[ch, tsl(t)], S1.all(), s2, ALU.mult)
                if ch == NCH - 1:
                    release(bms[t])

            part_a(0)
            for i in range(len(iters)):
                if i + 1 < len(iters):
                    part_a(i + 1)
                part_b(i)
            done(zslots[0])
            done(zslots[1])

        def fourier(l):
            base = l * LC
            A = tv(0, BF16, (2, 4, 512))
            U = tv(8192, F32, (S,))
            FO = tv(16384, BF16, (4, 512))
            t_u = U.ap
            pstep = t_u.ap[0][0]

            def rev(lo, hi):
                ap = bass.AP(t_u.tensor, t_u.offset + hi - 1, [[pstep, P], [-1, hi - lo]])
                return V(ap, U[lo:hi].ranges)

            for gp in range(2):
                slot = load_w(("in", l, 2 * gp, 2 * gp + 1))
                zslot = load_w(("in", l, 4 + 2 * gp, 5 + 2 * gp))
                for w in range(2):
                    g = 2 * gp + w
                    bg.pause()
                    for t in range(NT):
                        b = inproj(slot, w, t)
                        act(U[tsl(t)], b.all(), AF.Identity, bias=cc(base + g))
                    r1 = rev(1025, 2048)
                    r2 = rev(513, 1024)
                    r3 = rev(1537, 2048)
                    tt(DVE, r1, U[1:1024], r1, ALU.subtract)
                    pg.op(DVE, lambda: nc.vector.memset(FO[1, 0:1].ap, 0.0), writes=[FO[1, 0:1]])
                    pg.op(DVE, lambda: nc.vector.scalar_tensor_tensor(out=U[1:1024].ap, in0=U[1:1024].ap, scalar=2.0,
                                                                      in1=r1.ap, op0=ALU.mult, op1=ALU.subtract),
                          reads=[U[1:1024], r1], writes=[U[1:1024]])
                    pg.op(DVE, lambda: nc.vector.memset(FO[3, 0:1].ap, 0.0), writes=[FO[3, 0:1]])
                    tt(DVE, FO[0, 1:512], U[1:512], r2, ALU.add)
                    pg.op(DVE, lambda: nc.vector.tensor_copy(out=Ez[0:1].ap, in_=U[512:513].ap),
                          reads=[U[512:513]], writes=[Ez[0:1]])
                    tt(DVE, FO[2, 1:512], U[1:512], r2, ALU.subtract)
                    pg.op(DVE, lambda: nc.vector.tensor_copy(out=Ez2[0:1].ap, in_=U[1536:1537].ap),
                          reads=[U[1536:1537]], writes=[Ez2[0:1]])
                    tt(DVE, FO[1, 1:512], r3, U[1025:1536], ALU.subtract)
                    tt(DVE, FO[0, 0:1], U[0:1], U[1024:1025], ALU.add)
                    tt(DVE, FO[3, 1:512], r3, U[1025:1536], ALU.add)
                    tt(DVE, FO[2, 0:1], U[0:1], U[1024:1025], ALU.subtract)
                    bg.resume()
                    for t in range(NT):
                        b = inproj(zslot, w, t)
                        act(Y[g, tsl(t)], b.all(), AF.Silu, bias=cc(base + 4 + g))
                    for j in range(4):
                        b = nextbank()
                        mms = []
                        rd = [CC_T, MS_T]
                        for part in range(4):
                            lv = FO[part, j * 128:(j + 1) * 128]
                            tab = CC_T if part % 2 == 0 else MS_T
                            rd.append(lv)
                            special = (j == 0 and part % 2 == 1)
                            mms.append((b[part * 128:(part + 1) * 128].ap, lv.ap, tab.ap, True, not special))
                            if special:
                                ez = Ez if part == 1 else Ez2
                                tab2 = CC_T if part == 1 else MS_T
                                mms.append((b[part * 128:(part + 1) * 128].ap, ez.all().ap, tab2.ap, False, True))
                                rd.append(ez.all())
                        pg.mm_group(PE, mms, reads=rd, writes=[b.all()])
                        bg.tick(len(mms) * 0.5)
                        act(A[w, j], b.all(), AF.Copy)
                done(slot)
                done(zslot)
                bg.pause()
                for r in range(2):
                    for qt in range(2):
                        bks = [nextbank() for _ in range(2)]
                        for cs in range(2):
                            sl = load_dft((r * 2 + qt) * 2 + cs)
                            for w in range(2):
                                mms = []
                                rd = [sl.all()]
                                for j in range(4):
                                    lv = A[w, j, (2 * r + cs) * 128:(2 * r + cs + 1) * 128]
                                    rv = sl[j * TW:(j + 1) * TW]
                                    rd.append(lv)
                                    mms.append((bks[w].ap, lv.ap, rv.ap, cs == 0 and j == 0, cs == 1 and j == 3))
                                pg.mm_group(PE, mms, reads=rd, writes=[bks[w].all()])
                                bg.tick(4)
                            done(sl)
                        for w in range(2):
                            g = 2 * gp + w
                            yq = YQ[g, qt * TW:(qt + 1) * TW, r]
                            yv = V(yq.ap, Y[g, qt * 2 * TW:(qt + 1) * 2 * TW].ranges)
                            tt(DVE, yv, bks[w].all(), yv, ALU.mult)

                bg.resume()

        def shortconv(l):
            base = l * LC
            SW = base + 204
            SB = base + 216
            UB = tv(0, BF16, (S + 2,))
            DGS = [tv(4352, BF16, (3, 128)), tv(5120, BF16, (3, 128))]
            TC = tv(8192, F32, (2, TW))
            TZ = tv(12288, F32, (2, TW))
            n = 0
            todo = []

            def halo(which):
                v = UB[0:1] if which == 0 else UB[S + 1:S + 2]
                pg.op(DVE, lambda: nc.vector.memset(v.ap, 0.0), writes=[v])

            def build_dg(ch, k):
                dgk = DGS[ch % 2][k]
                wk = cc(SW + ch * 3 + k)
                pg.op(DVE, lambda: nc.vector.tensor_scalar(out=dgk.ap, in0=IDENT.ap, scalar1=wk.ap,
                                                           scalar2=None, op0=ALU.mult),
                      reads=[IDENT, wk], writes=[dgk])

            def one_setup():
                if todo:
                    todo.pop(0)()

            todo += [lambda: halo(0), lambda: build_dg(0, 0), lambda: build_dg(0, 1), lambda: build_dg(0, 2),
                     lambda: halo(1)]
            for ch in range(NCH):
                s1 = load_w(("in", l, 24 + ch, 28 + ch))
                s2 = load_w(("in", l, 20 + ch, 32 + ch))
                DG = DGS[ch % 2]

                def conv_tile(t, ch=ch, DG=DG):
                    while todo and ch == 0 and t == 0:
                        todo.pop(0)()
                    b = nextbank()
                    mms = []
                    rd = [DG.all()]
                    for k in range(3):
                        rv = UB[t * TW + k:t * TW + k + TW]
                        rd.append(rv)
                        mms.append((b.ap, DG[k].ap, rv.ap, k == 0, k == 2))
                    pg.mm_group(PE, mms, reads=rd, writes=[b.all()])
                    bg.tick(3)
                    stt(DVE, Y[ch, tsl(t)], b.all(), cc(SB + ch), Y[ch, tsl(t)], ALU.add, ALU.mult)

                for t in range(NT):
                    tc_, tz_ = TC[n % 2], TZ[n % 2]
                    n += 1
                    if t == 2 and ch + 1 < NCH:
                        todo += [lambda c=ch + 1: build_dg(c, 0), lambda c=ch + 1: build_dg(c, 1),
                                 lambda c=ch + 1: build_dg(c, 2)]
                    bcg = inproj(s1, 0, t)
                    act(tc_, bcg.all(), AF.Identity, bias=cc(base + 24 + ch))
                    bh = inproj(s1, 1, t)
                    stt(DVE, UB[1 + t * TW:1 + (t + 1) * TW], bh.all(), cc(base + 28 + ch), tc_, ALU.add, ALU.mult)
                    one_setup()
                    bz = inproj(s2, 1, t)
                    act(tz_, bz.all(), AF.Silu, bias=cc(base + 32 + ch))
                    bbg = inproj(s2, 0, t)
                    stt(DVE, Y[ch, tsl(t)], bbg.all(), cc(base + 20 + ch), tz_, ALU.add, ALU.mult)
                    one_setup()
                    if t >= 1:
                        conv_tile(t - 1)
                done(s1)
                done(s2)
                conv_tile(NT - 1)
            while todo:
                todo.pop(0)()

        def outproj(l, norm):
            stage_a, stage_b, tiles = norm
            for hf in range(2):
                oslots = [load_w(("out", l, 2 * hf + i)) for i in range(2)]
                ranges = [(t * TW, (t + 1) * TW) for t in range(NT)] if hf == 0 else tiles
                for i, (lo, hi) in enumerate(ranges):
                    for e in range(4):
                        ec = 4 * hf + e
                        bo = inproj(oslots[e // 2], e % 2, None, rhs=Mb, cols=(lo, hi))
                        tt(DVE, xT[ec, lo:hi], bo[0:hi - lo], xT[ec, lo:hi], ALU.add)
                    if hf == 1:
                        if i >= 1:
                            stage_a(i - 1)
                        if i >= 2:
                            stage_b(i - 2)
                done(oslots[0])
                done(oslots[1])
            n_t = len(tiles)
            stage_a(n_t - 1)
            stage_b(n_t - 2)
            stage_b(n_t - 1)

        rmsnorm_first(60)
        for l in range(depth):
            conformer_front(l)
            bg.quota = QUOTA["F"]
            fourier(l)
            bg.quota = QUOTA["Fg"]
            gating(l, 0, True, Y)
            bg.quota = QUOTA["S"]
            shortconv(l)
            bg.quota = QUOTA["Sg"]
            sg = gating_steps(l, 2, False, Y)
            for stp in sg[:-4]:
                stp()
            bg.drain()
            conformer_back(l, fillers=sg[-4:])
            gating(l, 1, False, CT)
            if l + 1 < depth:
                norm = rms_stages((l + 1) * LC + 60, True)
            elif final_norm:
                norm = rms_stages(DEPTH * LC, False,
                                  tiles=[(0, 512), (512, 1024), (1024, 1536), (1536, 1792), (1792, 2048)])
            else:
                norm = None
            outproj(l, norm)
        SP.wait((s_out, s_out.count))
        stats = {e.name: e.ninst for e in (PE, ACT, DVE, POOL, SP)}
    return nc, w_items, stats


def _chunk_lhsT(W, j, nk):
    blk = W[:, j * 128:(j + 1) * 128].reshape(nk, P, 128)
    return np.transpose(blk, (1, 0, 2))


def _build_ws(w_items, w_in, w_branch, w_out):
    ws = np.empty((len(w_items), P, SLOT_ELEMS), np.float32)
    for n, d in enumerate(w_items):
        if d[0] == "in":
            _, l, ja, jb = d
            it = np.stack([_chunk_lhsT(w_in[l], ja, KD), _chunk_lhsT(w_in[l], jb, KD)], axis=1)
        elif d[0] == "out":
            _, l, i = d
            it = np.stack([_chunk_lhsT(w_out[l], 2 * i, KD), _chunk_lhsT(w_out[l], 2 * i + 1, KD)], axis=1)
        else:
            _, l, br, half = d
            it = np.stack([_chunk_lhsT(w_branch[l, br], 4 * half + q, NCH) for q in range(4)], axis=1)
        ws[n] = it.reshape(P, SLOT_ELEMS)
    return ws


def _build_consts(norm_g, b_in, conv_c_w, conv_c_b, ln_c_g, ln_c_b, conv_s_w, conv_s_b, final_g):
    cst = np.zeros((P, DEPTH * LC + 10), np.float32)
    for l in range(DEPTH):
        b = l * LC
        cst[:, b:b + 60] = b_in[l].reshape(60, P).T
        cst[:, b + 60:b + 68] = norm_g[l].reshape(KD, P).T
        cst[:, b + 68:b + 192] = np.transpose(conv_c_w[l].reshape(CONF_K, NCH, P), (2, 1, 0)).reshape(P, NCH * CONF_K)
        cst[:, b + 192:b + 196] = conv_c_b[l].reshape(NCH, P).T
        cst[:, b + 196:b + 200] = ln_c_g[l].reshape(NCH, P).T
        cst[:, b + 200:b + 204] = ln_c_b[l].reshape(NCH, P).T
        cst[:, b + 204:b + 216] = np.transpose(conv_s_w[l].reshape(3, NCH, P), (2, 1, 0)).reshape(P, NCH * 3)
        cst[:, b + 216:b + 220] = conv_s_b[l].reshape(NCH, P).T
    cst[:, DEPTH * LC:DEPTH * LC + 8] = final_g.reshape(KD, P).T
    cst[:, DEPTH * LC + 8] = EPS
    return cst


_TABLES = {}


def _dft_tables():
    if "dft" in _TABLES:
        return _TABLES["dft"], _TABLES["ctab"]
    bf = ml_dtypes.bfloat16
    c = np.arange(128)
    ang = 2.0 * np.pi * ((c[:, None] * c[None, :]) % 128) / 128.0
    sc = 2.0 ** -9
    ctab = np.zeros((P, 512), np.float32)
    ctab[:, 0:128] = np.cos(ang) * sc
    ctab[:, 128:256] = -np.sin(ang) * sc
    ctab[:, 256:384] = 1.0
    ctab[:, 384:512] = np.eye(P)
    ctab = ctab.astype(bf)
    s = np.arange(512, dtype=np.int64)
    q = np.arange(S // 2, dtype=np.int64)
    dft = np.empty((8, P, 4, TW), np.float32)
    for r in range(2):
        sp = 2 * q + r
        ang = 2.0 * np.pi * ((s[:, None] * sp[None, :]) % S).astype(np.float64) / S
        Cs = np.cos(ang)
        Ss = np.sin(ang)
        Ss[0, :] = np.where(q % 2 == 0, 1.0, -1.0)
        for qt in range(2):
            for cs in range(2):
                T = Cs if cs == 0 else Ss
                blk = T[:, qt * TW:(qt + 1) * TW].reshape(4, P, TW)
                dft[(r * 2 + qt) * 2 + cs] = np.transpose(blk, (1, 0, 2))
    dft = dft.reshape(8, P, SLOT_ELEMS).astype(bf)
    _TABLES["dft"] = dft
    _TABLES["ctab"] = ctab
    return dft, ctab


_PROG = {}


def kernel(x, norm_g, w_in, b_in, conv_c_w, conv_c_b, ln_c_g, ln_c_b,
           conv_s_w, conv_s_b, w_branch, w_out, final_g):
    x = np.asarray(x, np.float32)
    w_in = np.asarray(w_in, np.float32)
    w_branch = np.asarray(w_branch, np.float32)
    w_out = np.asarray(w_out, np.float32)
    nc, w_items, _ = build_program()
    ws = _build_ws(w_items, w_in, w_branch, w_out)
    cst = _build_consts(np.asarray(norm_g, np.float32), np.asarray(b_in, np.float32),
                        np.asarray(conv_c_w, np.float32), np.asarray(conv_c_b, np.float32),
                        np.asarray(ln_c_g, np.float32), np.asarray(ln_c_b, np.float32),
                        np.asarray(conv_s_w, np.float32), np.asarray(conv_s_b, np.float32),
                        np.asarray(final_g, np.float32))
    dft, ctab = _dft_tables()
    B = x.shape[0]
    in_maps = []
    for b in range(B):
        xT = np.ascontiguousarray(np.transpose(x[b].reshape(NT, TW, KD, P), (0, 3, 2, 1))).reshape(NT, P, KD * TW)
        in_maps.append({"xT": xT, "ws": ws, "dft": dft, "cst": cst, "ctab": ctab})
    res = run_bass_kernel_spmd(nc, in_maps, core_ids=list(range(B)))
    out = np.empty((B, S, D), np.float32)
    for b in range(B):
        out[b] = np.transpose(res.results[b]["yT"].reshape(NT, P, KD, TW), (0, 3, 2, 1)).reshape(S, D)
    return out
```

```python
import bisect
from contextlib import ExitStack

import numpy as np
import ml_dtypes

import concourse.bass as bass
import concourse.mybir as mybir
from concourse.bass_utils import run_bass_kernel_spmd

F32 = mybir.dt.float32
BF16 = mybir.dt.bfloat16
U8 = mybir.dt.uint8
AF = mybir.ActivationFunctionType
ALU = mybir.AluOpType

P = 128
D = 1024
S = 2048
KD = 8
TW = 512
NT = 4
BW = 512
NCH = 4
IN_W = 7680
CONF_K = 31
DEPTH = 2
EPS = 1e-6
NSLOT = 4
SLOT_ELEMS = 2048
LC = 220
DSIZE = {F32: 4, BF16: 2, U8: 1}


class Sem:
    def __init__(self, h, name):
        self.h = h
        self.name = name
        self.count = 0


class Region:
    def __init__(self, size):
        self.los = [0]
        self.ivs = [[0, size, None, {}]]

    def _split(self, pos):
        i = bisect.bisect_right(self.los, pos) - 1
        iv = self.ivs[i]
        if iv[0] == pos or pos >= iv[1]:
            return
        new = [pos, iv[1], iv[2], dict(iv[3])]
        iv[1] = pos
        self.los.insert(i + 1, pos)
        self.ivs.insert(i + 1, new)

    def get(self, lo, hi):
        self._split(lo)
        self._split(hi)
        i = bisect.bisect_left(self.los, lo)
        out = []
        while i < len(self.ivs) and self.ivs[i][0] < hi:
            out.append(self.ivs[i])
            i += 1
        return out


class V:
    def __init__(self, ap, ranges):
        self.ap = ap
        self.ranges = ranges


class Buf:
    def __init__(self, region_name, base_ap, dtype, shape, byte_off):
        self.rn = region_name
        self.ap = base_ap
        self.dtype = dtype
        self.shape = tuple(shape)
        self.off = byte_off
        self.es = DSIZE[dtype]

    def __getitem__(self, idx):
        if not isinstance(idx, tuple):
            idx = (idx,)
        idx = tuple(idx) + (slice(None),) * (len(self.shape) - len(idx))
        ap = self.ap[(slice(None),) + idx]
        strides = []
        s = 1
        for n in reversed(self.shape):
            strides.append(s)
            s *= n
        strides = list(reversed(strides))
        ranges = [(0, 1)]
        offs = [0]
        for d, (ix, n) in enumerate(zip(idx, self.shape)):
            st = strides[d]
            if isinstance(ix, int):
                lo, hi = ix, ix + 1
            else:
                lo = 0 if ix.start is None else ix.start
                hi = n if ix.stop is None else ix.stop
            last = d == len(self.shape) - 1
            if last:
                offs = [(o + lo * st, (hi - lo) * st) for o in offs]
            else:
                offs = [o + i * st for o in offs for i in range(lo, hi)]
        rr = []
        for o, ln in offs:
            a = self.off + o * self.es
            b = a + ln * self.es
            if rr and rr[-1][2] == a and rr[-1][0] == self.rn:
                rr[-1] = (self.rn, rr[-1][1], b)
            else:
                rr.append((self.rn, a, b))
        return V(ap, rr)

    def all(self):
        return self[tuple(slice(None) for _ in self.shape)]


class Eng:
    def __init__(self, name, e, sem, self_sync=True):
        self.name = name
        self.e = e
        self.sem = sem
        self.seen = {}
        self.self_sync = self_sync
        self.ninst = 0

    def wait(self, ev):
        sem, val = ev
        if sem is self.sem and not self.self_sync:
            return
        if self.seen.get(sem, 0) >= val:
            return
        self.e.wait_ge(sem.h, val)
        self.seen[sem] = val


class Prog:
    def __init__(self, nc, es):
        self.nc = nc
        self.es = es
        self.regions = {}
        self.nsem = 0

    def new_sem(self, name):
        h = self.es.enter_context(self.nc.semaphore(name))
        return Sem(h, name)

    def sbuf(self, name, nbytes):
        t = self.es.enter_context(self.nc.sbuf_tensor(name, [P, nbytes], U8))
        self.regions[name] = Region(nbytes)
        return t

    def view(self, t, name, byte_off, dtype, shape):
        n = int(np.prod(shape))
        ap = t[:, byte_off:byte_off + n * DSIZE[dtype]]
        if dtype != U8:
            ap = ap.bitcast(dtype)
        if len(shape) == 2:
            ap = ap.rearrange("p (a b) -> p a b", a=shape[0])
        elif len(shape) == 3:
            ap = ap.rearrange("p (a b c) -> p a b c", a=shape[0], b=shape[1])
        elif len(shape) == 4:
            ap = ap.rearrange("p (a b c d) -> p a b c d", a=shape[0], b=shape[1], c=shape[2])
        return Buf(name, ap, dtype, shape, byte_off)

    def psum(self, name):
        t = self.es.enter_context(self.nc.psum_tensor(name, [P, TW], F32))
        self.regions[name] = Region(TW * 4)
        return Buf(name, t[:, :], F32, (TW,), 0)

    def _deps(self, reads, writes):
        deps = {}

        def add(ev):
            if ev is None:
                return
            s, v = ev
            if deps.get(s, 0) < v:
                deps[s] = v

        for v in reads:
            for rn, lo, hi in v.ranges:
                for iv in self.regions[rn].get(lo, hi):
                    add(iv[2])
        for v in writes:
            for rn, lo, hi in v.ranges:
                for iv in self.regions[rn].get(lo, hi):
                    add(iv[2])
                    for s, val in iv[3].items():
                        add((s, val))
        return deps

    def _commit(self, reads, writes, ev):
        s, val = ev
        self.seq = getattr(self, "seq", 0) + 1
        touch = getattr(self, "touch", None)
        if touch is None:
            touch = self.touch = {}
        for v in list(reads) + list(writes):
            for rn, lo, hi in v.ranges:
                touch[rn] = self.seq
        for v in reads:
            for rn, lo, hi in v.ranges:
                for iv in self.regions[rn].get(lo, hi):
                    if iv[3].get(s, 0) < val:
                        iv[3][s] = val
        for v in writes:
            for rn, lo, hi in v.ranges:
                for iv in self.regions[rn].get(lo, hi):
                    iv[2] = ev
                    iv[3] = {}

    def op(self, eng, build, reads=(), writes=()):
        deps = self._deps(reads, writes)
        for s, v in deps.items():
            eng.wait((s, v))
        inst = build()
        eng.sem.count += 1
        inst.then_inc(eng.sem.h, 1)
        eng.ninst += 1
        ev = (eng.sem, eng.sem.count)
        self._commit(reads, writes, ev)
        return ev

    def mm_group(self, eng, mms, reads, writes):
        deps = self._deps(reads, writes)
        for s, v in deps.items():
            eng.wait((s, v))
        inst = None
        for (o, l, r, st, sp) in mms:
            inst = self.nc.tensor.matmul(o, lhsT=l, rhs=r, start=st, stop=sp)
            eng.ninst += 1
        eng.sem.count += 1
        inst.then_inc(eng.sem.h, 1)
        ev = (eng.sem, eng.sem.count)
        self._commit(reads, writes, ev)
        return ev

    def dma(self, eng, sem, out_ap, in_ap, reads=(), writes=()):
        deps = self._deps(reads, writes)
        for s, v in deps.items():
            eng.wait((s, v))
        inst = eng.e.dma_start(out=out_ap, in_=in_ap)
        sem.count += 16
        inst.then_inc(sem.h, 16)
        ev = (sem, sem.count)
        self._commit(reads, writes, ev)
        return ev


def build_program(depth=DEPTH, n_witems=None, final_norm=True):
    nc = bass.Bass("TRN2", target_bir_lowering=False)
    w_items = []

    xin = nc.dram_tensor("xT", [NT, P, KD * TW], F32, kind="ExternalInput").ap()
    n_w = n_witems if n_witems is not None else 40 * depth
    ws = nc.dram_tensor("ws", [n_w, P, SLOT_ELEMS], F32, kind="ExternalInput").ap()
    dft = nc.dram_tensor("dft", [8, P, SLOT_ELEMS], BF16, kind="ExternalInput").ap()
    cst_d = nc.dram_tensor("cst", [P, DEPTH * LC + 10], F32, kind="ExternalInput").ap()
    ctab_d = nc.dram_tensor("ctab", [P, 512], BF16, kind="ExternalInput").ap()
    yout = nc.dram_tensor("yT", [NT, P, KD * TW], F32, kind="ExternalOutput").ap()

    with ExitStack() as es:
        pg = Prog(nc, es)
        t_x = pg.sbuf("xT_r", KD * S * 4)
        t_h = pg.sbuf("hT_r", KD * S * 2)
        t_m = pg.sbuf("M_r", KD * S * 2)
        t_y = pg.sbuf("Y_r", NCH * S * 2)
        t_ring = pg.sbuf("ring_r", NSLOT * SLOT_ELEMS * 2)
        TEMP_BYTES = 20 * 1024
        t_tmp = pg.sbuf("tmp_r", TEMP_BYTES)
        t_cf = pg.sbuf("CF_r", NCH * S * 2 + S * 4)
        NCST = DEPTH * LC + 10
        t_cst = pg.sbuf("cst_r", NCST * 4)
        t_ctab = pg.sbuf("ctab_r", 512 * 2 + 2 * 128 * 2 + 2 * 128 * 2)

        xT = pg.view(t_x, "xT_r", 0, F32, (KD, S))
        hT = pg.view(t_h, "hT_r", 0, BF16, (KD, S))
        Mb = pg.view(t_m, "M_r", 0, BF16, (KD, S))
        CT = pg.view(t_cf, "CF_r", 0, BF16, (NCH, S))
        ACCF = pg.view(t_cf, "CF_r", NCH * S * 2, F32, (S,))
        Y = pg.view(t_y, "Y_r", 0, BF16, (NCH, S))
        slots = [pg.view(t_ring, "ring_r", i * SLOT_ELEMS * 2, BF16, (SLOT_ELEMS,)) for i in range(NSLOT)]
        cst = pg.view(t_cst, "cst_r", 0, F32, (NCST,))
        ctab = pg.view(t_ctab, "ctab_r", 0, BF16, (512,))
        Ez = pg.view(t_ctab, "ctab_r", 1024, BF16, (128,))
        Ez2 = pg.view(t_ctab, "ctab_r", 1280, BF16, (128,))
        DG2 = pg.view(t_ctab, "ctab_r", 1536, BF16, (2, 128))
        YQ = pg.view(t_y, "Y_r", 0, BF16, (NCH, S // 2, 2))

        banks = [pg.psum(f"ps{i}") for i in range(8)]
        bank_i = [0]

        held = set()

        def nextbank():
            touch = getattr(pg, "touch", {})
            best = None
            for b in banks:
                if id(b) in held:
                    continue
                key = touch.get(b.rn, 0)
                if best is None or key < best[0]:
                    best = (key, b)
            b = best[1]
            pg.seq = getattr(pg, "seq", 0) + 1
            if not hasattr(pg, "touch"):
                pg.touch = {}
            pg.touch[b.rn] = pg.seq
            return b

        def hold(b):
            held.add(id(b))

        def release(b):
            held.discard(id(b))

        PE = Eng("pe", nc.tensor, pg.new_sem("s_pe"), self_sync=False)
        ACT = Eng("act", nc.scalar, pg.new_sem("s_act"))
        DVE = Eng("dve", nc.vector, pg.new_sem("s_dve"))
        POOL = Eng("pool", nc.gpsimd, pg.new_sem("s_pool"))
        SP = Eng("sp", nc.sync, pg.new_sem("s_sp"))
        slot_sems = [pg.new_sem(f"s_slot{i}") for i in range(NSLOT)]
        slot_sems_d = [pg.new_sem(f"s_dslot{i}") for i in range(NSLOT)]
        s_in = [pg.new_sem(f"s_in{i}") for i in range(NT)]
        s_out = pg.new_sem("s_out")
        s_c = [pg.new_sem("s_c0"), pg.new_sem("s_c1")]
        ring_i = [0]

        def cc(col):
            return cst[col:col + 1]

        def tv(off, dtype, shape):
            n = int(np.prod(shape)) * DSIZE[dtype]
            assert off + n <= TEMP_BYTES, (off, n)
            return pg.view(t_tmp, "tmp_r", off, dtype, shape)

        live = [False] * NSLOT
        ev_x0_box = [None] * NT

        def pick_slot():
            best, bestv = None, None
            for i in range(NSLOT):
                if live[i]:
                    continue
                v = 0
                for iv in pg.regions["ring_r"].get(i * SLOT_ELEMS * 2, (i + 1) * SLOT_ELEMS * 2):
                    v = max(v, iv[3].get(PE.sem, 0))
                if best is None or v < bestv:
                    best, bestv = i, v
            assert best is not None, "ring: no free slot"
            live[best] = True
            return best

        def done(sl):
            live[slots.index(sl)] = False

        def load_w(desc):
            if len(w_items) == 1:
                POOL.wait(ev_x0_box[2])
            i = pick_slot()
            sl = slots[i]
            idx = len(w_items)
            w_items.append(desc)
            pg.dma(POOL, slot_sems[i], sl.ap, ws[idx], writes=[sl.all()])
            return sl

        def load_dft(idx):
            i = pick_slot()
            sl = slots[i]
            pg.dma(SP, slot_sems_d[i], sl.ap, dft[idx], writes=[sl.all()])
            return sl

        def act(out_v, in_v, func, bias=None, scale=None, extra_reads=()):
            kw = {}
            if bias is not None:
                kw["bias"] = bias.ap
            if scale is not None:
                kw["scale"] = scale.ap if isinstance(scale, V) else scale
            rd = [in_v] + list(extra_reads)
            if bias is not None:
                rd.append(bias)
            if isinstance(scale, V):
                rd.append(scale)
            return pg.op(ACT, lambda: nc.scalar.activation(out=out_v.ap, in_=in_v.ap, func=func, **kw),
                         reads=rd, writes=[out_v])

        def tt(eng, out_v, a_v, b_v, op):
            return pg.op(eng, lambda: eng.e.tensor_tensor(out=out_v.ap, in0=a_v.ap, in1=b_v.ap, op=op),
                         reads=[a_v, b_v], writes=[out_v])

        def stt(eng, out_v, a_v, sc_v, b_v, op0, op1):
            return pg.op(eng, lambda: eng.e.scalar_tensor_tensor(out=out_v.ap, in0=a_v.ap, scalar=sc_v.ap,
                                                                 in1=b_v.ap, op0=op0, op1=op1),
                         reads=[a_v, sc_v, b_v], writes=[out_v])

        def ts2(eng, out_v, a_v, s1_v, s2_v, op0, op1):
            return pg.op(eng, lambda: eng.e.tensor_scalar(out=out_v.ap, in0=a_v.ap, scalar1=s1_v.ap,
                                                          scalar2=s2_v.ap, op0=op0, op1=op1),
                         reads=[a_v, s1_v, s2_v], writes=[out_v])

        def tsl(t):
            return slice(t * TW, (t + 1) * TW)

        def bg_tick(n):
            bg.tick(n)

        def inproj(slot, which, t, nk=KD, rhs=None, cols=None):
            rhs = hT if rhs is None else rhs
            b = nextbank()
            mms = []
            rd = []
            per = nk * 128
            csl = tsl(t) if cols is None else slice(cols[0], cols[1])
            ov = b.all() if cols is None else b[0:cols[1] - cols[0]]
            for k in range(nk):
                lo = which * per + k * 128
                lv = slot[lo:lo + 128]
                rv = rhs[k, csl]
                mms.append((ov.ap, lv.ap, rv.ap, k == 0, k == nk - 1))
                rd.append(rv)
            rd.append(slot[which * per:(which + 1) * per])
            pg.mm_group(PE, mms, reads=rd, writes=[ov])
            bg_tick(nk)
            return b

        xin_r = xin.rearrange("t p (k n) -> t p k n", k=KD)
        yout_r = yout.rearrange("t p (k n) -> t p k n", k=KD)
        ev_x0 = None
        for t in range(NT):
            v = xT[:, tsl(t)]
            ev = pg.dma(SP, s_in[t], v.ap, xin_r[t], writes=[v])
            ev_x0_box[t] = ev
            if t == 0:
                ev_x0 = ev
                pg.dma(SP, s_c[0], cst.ap, cst_d[:, :], writes=[cst.all()])
                pg.dma(SP, s_c[1], ctab.ap, ctab_d[:, :], writes=[ctab.all()])
            if t == 1:
                SP.wait(ev_x0)
        CC_T = ctab[0:128]
        MS_T = ctab[128:256]
        ONES = ctab[256:384]
        IDENT = ctab[384:512]
        pg.op(DVE, lambda: nc.vector.memset(Ez.ap, 0.0), writes=[Ez.all()])
        pg.op(DVE, lambda: nc.vector.memset(Ez2.ap, 0.0), writes=[Ez2.all()])
        EPSC = DEPTH * LC + 8
        ZEROC = DEPTH * LC + 9

        class BG:
            def __init__(self):
                self.q = []
                self.credit = 0.0
                self.quota = 13.0
                self.paused = False

            def add(self, th):
                self.q.append(th)

            def tick(self, n):
                if not self.q:
                    self.credit = 0.0
                    return
                self.credit += n
                if self.paused:
                    return
                while self.q and self.credit >= self.quota:
                    self.credit -= self.quota
                    self.q.pop(0)()

            def pause(self):
                self.paused = True

            def resume(self):
                self.paused = False
                self.tick(0)

            def drain(self):
                while self.q:
                    self.q.pop(0)()
                self.credit = 0.0

        bg = BG()
        KPE = [15, 14, 16]
        dg_i = [0]
        QUOTA = {"front": 12.0, "F": 15.0, "Fg": 13.0, "S": 15.0, "Sg": 15.0}

        def rstd_from(dst, src_v, scale=1.0):
            act(dst, src_v, AF.Ln, bias=cc(EPSC), scale=scale)
            act(dst, dst, AF.Exp, scale=-0.5)

        def rms_stages(gcol, to_hT, tiles=None):
            SQ = tv(0, BF16, (2, KD, TW))
            RS = tv(16384, F32, (2, TW))
            bks = {}
            tiles = [(t * TW, (t + 1) * TW) for t in range(NT)] if tiles is None else tiles
            n_t = len(tiles)

            def stage_a(i):
                lo, hi = tiles[i]
                w = hi - lo
                sq = SQ[i % 2, :, 0:w]
                pg.op(ACT, lambda: nc.scalar.activation(out=sq.ap, in_=xT[:, lo:hi].ap, func=AF.Square),
                      reads=[xT[:, lo:hi]], writes=[sq])
                b = nextbank()
                mms = [(b[0:w].ap, ONES.ap, SQ[i % 2, k, 0:w].ap, k == 0, k == KD - 1) for k in range(KD)]
                pg.mm_group(PE, mms, reads=[ONES, sq], writes=[b[0:w]])
                bks[i] = b
                hold(b)

            def stage_b(i):
                lo, hi = tiles[i]
                w = hi - lo
                rs = RS[i % 2, 0:w]
                rstd_from(rs, bks[i][0:w], scale=1.0 / D)
                release(bks[i])
                for k in range(KD):
                    if to_hT:
                        stt(DVE, hT[k, lo:hi], xT[k, lo:hi], cc(gcol + k), rs, ALU.mult, ALU.mult)
                    else:
                        stt(DVE, xT[k, lo:hi], xT[k, lo:hi], cc(gcol + k), rs, ALU.mult, ALU.mult)
                        if k % 2 == 1:
                            v = xT[k - 1:k + 1, lo:hi]
                            pg.dma(SP, s_out, yout_r[lo // TW, :, k - 1:k + 1, lo % TW:lo % TW + w], v.ap, reads=[v])

            return stage_a, stage_b, tiles

        def rmsnorm_first(gcol):
            stage_a, stage_b, tiles = rms_stages(gcol, True)
            n_t = len(tiles)
            stage_a(0)
            stage_b(0)
            stage_a(1)
            for i in range(1, n_t):
                if i + 1 < n_t:
                    stage_a(i + 1)
                stage_b(i)

        def gating_steps(l, br, first, ysrc):
            base = l * LC
            GT = tv(14336, F32, (2, TW))
            TM = tv(18432, BF16, (2, TW))
            st = {"n": 0}
            steps = []
            for half in range(2):
                for i in range(2):
                    for w in range(2):
                        for t in range(NT):
                            def step(half=half, i=i, w=w, t=t):
                                j0 = 36 + 8 * br + 4 * half + 2 * i
                                dc = 4 * half + 2 * i + w
                                if t == 0 and w == 0 and i == 0:
                                    st["b"] = load_w(("br", l, br, half))
                                if t == 0 and w == 0:
                                    st["g"] = load_w(("in", l, j0, j0 + 1))
                                last_g = (t == NT - 1 and w == 1)
                                last_b = last_g and i == 1
                                n = st["n"]
                                st["n"] += 1
                                bgk = inproj(st["g"], w, t)
                                gt = GT[n % 2]
                                act(gt, bgk.all(), AF.Sigmoid, bias=cc(base + j0 + w))
                                by = inproj(st["b"], dc % 4, t, nk=NCH, rhs=ysrc)
                                if first:
                                    tt(DVE, Mb[dc, tsl(t)], by.all(), gt, ALU.mult)
                                else:
                                    tm = TM[n % 2]
                                    tt(DVE, tm, by.all(), gt, ALU.mult)
                                    tt(DVE, Mb[dc, tsl(t)], tm, Mb[dc, tsl(t)], ALU.add)
                                if last_g:
                                    done(st["g"])
                                if last_b:
                                    done(st["b"])
                            steps.append(step)
            return steps

        def gating(l, br, first, ysrc):
            for stp in gating_steps(l, br, first, ysrc):
                stp()

        def conformer_front(l):
            base = l * LC
            CW = base + 68
            CB = base + 192
            SG = tv(0, F32, (2, TW))
            n = 0
            bg.quota = QUOTA["front"]
            for ch in range(NCH):
                slot = load_w(("in", l, 12 + ch, 8 + ch))
                for t in range(NT):
                    bb = inproj(slot, 0, t)
                    sg = SG[n % 2]
                    n += 1
                    act(sg, bb.all(), AF.Sigmoid, bias=cc(base + 12 + ch))
                    ba = inproj(slot, 1, t)
                    stt(DVE, CT[ch, tsl(t)], ba.all(), cc(base + 8 + ch), sg, ALU.add, ALU.mult)
                done(slot)
                st = {}

                def init(ch=ch):
                    ts2(DVE, ACCF.all(), CT[ch], cc(ZEROC), cc(CB + ch), ALU.mult, ALU.add)

                def tap(k, ch=ch):
                    d = k - 15
                    lo = max(0, -d)
                    hi = S - max(0, d)
                    stt(DVE, ACCF[lo:hi], CT[ch, lo + d:hi + d], cc(CW + ch * 31 + k), ACCF[lo:hi],
                        ALU.mult, ALU.add)

                def pe_part(ch=ch, st=st):
                    bks = [nextbank() for _ in range(NT)]
                    for b in bks:
                        hold(b)
                    st["bks"] = bks
                    for i, k in enumerate(KPE):
                        dg = DG2[dg_i[0] % 2]
                        dg_i[0] += 1
                        wk = cc(CW + ch * 31 + k)
                        pg.op(ACT, lambda: nc.scalar.activation(out=dg.ap, in_=IDENT.ap, func=AF.Copy, scale=wk.ap),
                              reads=[IDENT, wk], writes=[dg])
                        d = k - 15
                        mms, rd, wr = [], [dg], []
                        for t in range(NT):
                            clo = max(0, -(t * TW + d))
                            chi = min(TW, S - t * TW - d)
                            rv = CT[ch, t * TW + clo + d:t * TW + chi + d]
                            ov = bks[t][clo:chi]
                            rd.append(rv)
                            wr.append(ov)
                            mms.append((ov.ap, dg.ap, rv.ap, i == 0, i == len(KPE) - 1))
                        pg.mm_group(PE, mms, reads=rd, writes=wr)

                def merge(ch=ch, st=st):
                    for t in range(NT):
                        b = st["bks"][t]
                        tt(DVE, CT[ch, tsl(t)], b.all(), ACCF[tsl(t)], ALU.add)
                        release(b)

                dve_taps = [k for k in range(CONF_K) if k not in KPE]
                bg.add(init)
                for i, k in enumerate(dve_taps):
                    if i == len(dve_taps) - 6:
                        bg.add(pe_part)
                    bg.add(lambda k=k, tap=tap: tap(k))
                bg.add(merge)

        def conformer_back(l, fillers=()):
            base = l * LC
            LG = base + 196
            LB = base + 200
            VS = tv(0, BF16, (NCH, TW))
            MV = tv(4096, F32, (NT, TW))
            T1s = [tv(12288, F32, (TW,)), tv(0, F32, (TW,))]
            S1s = [tv(14336, F32, (TW,)), tv(2048, F32, (TW,))]
            S2 = tv(16384, F32, (2, TW))
            bms = []
            fillers = list(fillers)
            for t in range(NT):
                if fillers:
                    fillers.pop(0)()
                pg.op(ACT, lambda: nc.scalar.activation(out=VS.ap, in_=CT[:, tsl(t)].ap, func=AF.Square),
                      reads=[CT[:, tsl(t)]], writes=[VS.all()])
                bm = nextbank()
                pg.mm_group(PE, [(bm.ap, ONES.ap, CT[ch, tsl(t)].ap, ch == 0, ch == NCH - 1) for ch in range(NCH)],
                            reads=[ONES] + [CT[ch, tsl(t)] for ch in range(NCH)], writes=[bm.all()])
                hold(bm)
                bms.append(bm)
                be = nextbank()
                pg.mm_group(PE, [(be.ap, ONES.ap, VS[ch].ap, ch == 0, ch == NCH - 1) for ch in range(NCH)],
                            reads=[ONES, VS.all()], writes=[be.all()])
                act(MV[t], bm.all(), AF.Square, scale=1.0 / BW)
                mvt = MV[t]
                pg.op(DVE, lambda: nc.vector.scalar_tensor_tensor(out=mvt.ap, in0=be.ap, scalar=1.0 / BW, in1=mvt.ap,
                                                                  op0=ALU.mult, op1=ALU.subtract),
                      reads=[be.all(), mvt], writes=[mvt])
                pg.op(DVE, lambda: nc.vector.tensor_scalar_max(out=MV[t].ap, in0=MV[t].ap, scalar1=0.0),
                      reads=[MV[t]], writes=[MV[t]])
            while fillers:
                fillers.pop(0)()
            for t in range(NT):
                act(MV[t], MV[t], AF.Ln, bias=cc(EPSC), scale=1.0)
            for t in range(NT):
                act(MV[t], MV[t], AF.Exp, scale=-0.5)
            zslots = [load_w(("in", l, 16, 17)), load_w(("in", l, 18, 19))]
            iters = [(t, ch) for t in range(NT) for ch in range(NCH)]

            def part_a(i):
                t, ch = iters[i]
                bm = bms[t]
                s2, T1, S1 = S2[i % 2], T1s[i % 2], S1s[i % 2]
                bz = inproj(zslots[ch // 2], ch % 2, t)
                act(s2, bz.all(), AF.Silu, bias=cc(base + 16 + ch))
                ctv = CT[ch, tsl(t)]
                pg.op(DVE, lambda: nc.vector.scalar_tensor_tensor(out=T1.ap, in0=bm.ap, scalar=-1.0 / BW, in1=ctv.ap,
                                                                  op0=ALU.mult, op1=ALU.add),
                      reads=[bm.all(), ctv], writes=[T1.all()])
                tt(DVE, T1.all(), T1.all(), MV[t], ALU.mult)
                act(S1.all(), T1.all(), AF.Silu, bias=cc(LB + ch), scale=cc(LG + ch))

            def part_b(i):
                t, ch = iters[i]
                s2, S1 = S2[i % 2], S1s[i % 2]
                tt(DVE, CT[ch, tsl(t)], S1.all(), s2, ALU.mult)
                if ch == NCH - 1:
                    release(bms[t])

            part_a(0)
            for i in range(len(iters)):
                if i + 1 < len(iters):
                    part_a(i + 1)
                part_b(i)
            done(zslots[0])
            done(zslots[1])

        def fourier(l):
            base = l * LC
            A = tv(0, BF16, (2, 4, 512))
            U = tv(8192, F32, (S,))
            FO = tv(16384, BF16, (4, 512))
            t_u = U.ap
            pstep = t_u.ap[0][0]

            def rev(lo, hi):
                ap = bass.AP(t_u.tensor, t_u.offset + hi - 1, [[pstep, P], [-1, hi - lo]])
                return V(ap, U[lo:hi].ranges)

            for gp in range(2):
                slot = load_w(("in", l, 2 * gp, 2 * gp + 1))
                zslot = load_w(("in", l, 4 + 2 * gp, 5 + 2 * gp))
                for w in range(2):
                    g = 2 * gp + w
                    bg.pause()
                    for t in range(NT):
                        b = inproj(slot, w, t)
                        act(U[tsl(t)], b.all(), AF.Identity, bias=cc(base + g))
                    r1 = rev(1025, 2048)
                    r2 = rev(513, 1024)
                    r3 = rev(1537, 2048)
                    tt(DVE, r1, U[1:1024], r1, ALU.subtract)
                    pg.op(DVE, lambda: nc.vector.memset(FO[1, 0:1].ap, 0.0), writes=[FO[1, 0:1]])
                    pg.op(DVE, lambda: nc.vector.scalar_tensor_tensor(out=U[1:1024].ap, in0=U[1:1024].ap, scalar=2.0,
                                                                      in1=r1.ap, op0=ALU.mult, op1=ALU.subtract),
                          reads=[U[1:1024], r1], writes=[U[1:1024]])
                    pg.op(DVE, lambda: nc.vector.memset(FO[3, 0:1].ap, 0.0), writes=[FO[3, 0:1]])
                    tt(DVE, FO[0, 1:512], U[1:512], r2, ALU.add)
                    pg.op(DVE, lambda: nc.vector.tensor_copy(out=Ez[0:1].ap, in_=U[512:513].ap),
                          reads=[U[512:513]], writes=[Ez[0:1]])
                    tt(DVE, FO[2, 1:512], U[1:512], r2, ALU.subtract)
                    pg.op(DVE, lambda: nc.vector.tensor_copy(out=Ez2[0:1].ap, in_=U[1536:1537].ap),
                          reads=[U[1536:1537]], writes=[Ez2[0:1]])
                    tt(DVE, FO[1, 1:512], r3, U[1025:1536], ALU.subtract)
                    tt(DVE, FO[0, 0:1], U[0:1], U[1024:1025], ALU.add)
                    tt(DVE, FO[3, 1:512], r3, U[1025:1536], ALU.add)
                    tt(DVE, FO[2, 0:1], U[0:1], U[1024:1025], ALU.subtract)
                    bg.resume()
                    for t in range(NT):
                        b = inproj(zslot, w, t)
                        act(Y[g, tsl(t)], b.all(), AF.Silu, bias=cc(base + 4 + g))
                    for j in range(4):
                        b = nextbank()
                        mms = []
                        rd = [CC_T, MS_T]
                        for part in range(4):
                            lv = FO[part, j * 128:(j + 1) * 128]
                            tab = CC_T if part % 2 == 0 else MS_T
                            rd.append(lv)
                            special = (j == 0 and part % 2 == 1)
                            mms.append((b[part * 128:(part + 1) * 128].ap, lv.ap, tab.ap, True, not special))
                            if special:
                                ez = Ez if part == 1 else Ez2
                                tab2 = CC_T if part == 1 else MS_T
                                mms.append((b[part * 128:(part + 1) * 128].ap, ez.all().ap, tab2.ap, False, True))
                                rd.append(ez.all())
                        pg.mm_group(PE, mms, reads=rd, writes=[b.all()])
                        bg.tick(len(mms) * 0.5)
                        act(A[w, j], b.all(), AF.Copy)
                done(slot)
                done(zslot)
                for r in range(2):
                    for qt in range(2):
                        bks = [nextbank() for _ in range(2)]
                        for cs in range(2):
                            sl = load_dft((r * 2 + qt) * 2 + cs)
                            for w in range(2):
                                mms = []
                                rd = [sl.all()]
                                for j in range(4):
                                    lv = A[w, j, (2 * r + cs) * 128:(2 * r + cs + 1) * 128]
                                    rv = sl[j * TW:(j + 1) * TW]
                                    rd.append(lv)
                                    mms.append((bks[w].ap, lv.ap, rv.ap, cs == 0 and j == 0, cs == 1 and j == 3))
                                pg.mm_group(PE, mms, reads=rd, writes=[bks[w].all()])
                                bg.tick(4)
                            done(sl)
                        for w in range(2):
                            g = 2 * gp + w
                            yq = YQ[g, qt * TW:(qt + 1) * TW, r]
                            yv = V(yq.ap, Y[g, qt * 2 * TW:(qt + 1) * 2 * TW].ranges)
                            tt(DVE, yv, bks[w].all(), yv, ALU.mult)

        def shortconv(l):
            base = l * LC
            SW = base + 204
            SB = base + 216
            UB = tv(0, BF16, (S + 2,))
            DGS = [tv(4352, BF16, (3, 128)), tv(5120, BF16, (3, 128))]
            TC = tv(8192, F32, (2, TW))
            TZ = tv(12288, F32, (2, TW))
            n = 0
            todo = []

            def halo(which):
                v = UB[0:1] if which == 0 else UB[S + 1:S + 2]
                pg.op(DVE, lambda: nc.vector.memset(v.ap, 0.0), writes=[v])

            def build_dg(ch, k):
                dgk = DGS[ch % 2][k]
                wk = cc(SW + ch * 3 + k)
                pg.op(DVE, lambda: nc.vector.tensor_scalar(out=dgk.ap, in0=IDENT.ap, scalar1=wk.ap,
                                                           scalar2=None, op0=ALU.mult),
                      reads=[IDENT, wk], writes=[dgk])

            def one_setup():
                if todo:
                    todo.pop(0)()

            todo += [lambda: halo(0), lambda: build_dg(0, 0), lambda: build_dg(0, 1), lambda: build_dg(0, 2),
                     lambda: halo(1)]
            for ch in range(NCH):
                s1 = load_w(("in", l, 24 + ch, 28 + ch))
                s2 = load_w(("in", l, 20 + ch, 32 + ch))
                DG = DGS[ch % 2]

                def conv_tile(t, ch=ch, DG=DG):
                    while todo and ch == 0 and t == 0:
                        todo.pop(0)()
                    b = nextbank()
                    mms = []
                    rd = [DG.all()]
                    for k in range(3):
                        rv = UB[t * TW + k:t * TW + k + TW]
                        rd.append(rv)
                        mms.append((b.ap, DG[k].ap, rv.ap, k == 0, k == 2))
                    pg.mm_group(PE, mms, reads=rd, writes=[b.all()])
                    bg.tick(3)
                    stt(DVE, Y[ch, tsl(t)], b.all(), cc(SB + ch), Y[ch, tsl(t)], ALU.add, ALU.mult)

                for t in range(NT):
                    tc_, tz_ = TC[n % 2], TZ[n % 2]
                    n += 1
                    if t == 2 and ch + 1 < NCH:
                        todo += [lambda c=ch + 1: build_dg(c, 0), lambda c=ch + 1: build_dg(c, 1),
                                 lambda c=ch + 1: build_dg(c, 2)]
                    bcg = inproj(s1, 0, t)
                    act(tc_, bcg.all(), AF.Identity, bias=cc(base + 24 + ch))
                    bh = inproj(s1, 1, t)
                    stt(DVE, UB[1 + t * TW:1 + (t + 1) * TW], bh.all(), cc(base + 28 + ch), tc_, ALU.add, ALU.mult)
                    one_setup()
                    bz = inproj(s2, 1, t)
                    act(tz_, bz.all(), AF.Silu, bias=cc(base + 32 + ch))
                    bbg = inproj(s2, 0, t)
                    stt(DVE, Y[ch, tsl(t)], bbg.all(), cc(base + 20 + ch), tz_, ALU.add, ALU.mult)
                    one_setup()
                    if t >= 1:
                        conv_tile(t - 1)
                done(s1)
                done(s2)
                conv_tile(NT - 1)
            while todo:
                todo.pop(0)()

        def outproj(l, norm):
            stage_a, stage_b, tiles = norm
            for hf in range(2):
                oslots = [load_w(("out", l, 2 * hf + i)) for i in range(2)]
                ranges = [(t * TW, (t + 1) * TW) for t in range(NT)] if hf == 0 else tiles
                for i, (lo, hi) in enumerate(ranges):
                    for e in range(4):
                        ec = 4 * hf + e
                        bo = inproj(oslots[e // 2], e % 2, None, rhs=Mb, cols=(lo, hi))
                        tt(DVE, xT[ec, lo:hi], bo[0:hi - lo], xT[ec, lo:hi], ALU.add)
                    if hf == 1:
                        if i >= 1:
                            stage_a(i - 1)
                        if i >= 2:
                            stage_b(i - 2)
                done(oslots[0])
                done(oslots[1])
            n_t = len(tiles)
            stage_a(n_t - 1)
            stage_b(n_t - 2)
            stage_b(n_t - 1)

        rmsnorm_first(60)
        for l in range(depth):
            conformer_front(l)
            bg.quota = QUOTA["F"]
            fourier(l)
            bg.quota = QUOTA["Fg"]
            gating(l, 0, True, Y)
            bg.quota = QUOTA["S"]
            shortconv(l)
            bg.quota = QUOTA["Sg"]
            sg = gating_steps(l, 2, False, Y)
            for stp in sg[:-4]:
                stp()
            bg.drain()
            conformer_back(l, fillers=sg[-4:])
            gating(l, 1, False, CT)
            if l + 1 < depth:
                norm = rms_stages((l + 1) * LC + 60, True)
            elif final_norm:
                norm = rms_stages(DEPTH * LC, False,
                                  tiles=[(0, 512), (512, 1024), (1024, 1536), (1536, 1792), (1792, 2048)])
            else:
                norm = None
            outproj(l, norm)
        SP.wait((s_out, s_out.count))
        stats = {e.name: e.ninst for e in (PE, ACT, DVE, POOL, SP)}
    return nc, w_items, stats


def _chunk_lhsT(W, j, nk):
    blk = W[:, j * 128:(j + 1) * 128].reshape(nk, P, 128)
    return np.transpose(blk, (1, 0, 2))


def _build_ws(w_items, w_in, w_branch, w_out):
    ws = np.empty((len(w_items), P, SLOT_ELEMS), np.float32)
    for n, d in enumerate(w_items):
        if d[0] == "in":
            _, l, ja, jb = d
            it = np.stack([_chunk_lhsT(w_in[l], ja, KD), _chunk_lhsT(w_in[l], jb, KD)], axis=1)
        elif d[0] == "out":
            _, l, i = d
            it = np.stack([_chunk_lhsT(w_out[l], 2 * i, KD), _chunk_lhsT(w_out[l], 2 * i + 1, KD)], axis=1)
        else:
            _, l, br, half = d
            it = np.stack([_chunk_lhsT(w_branch[l, br], 4 * half + q, NCH) for q in range(4)], axis=1)
        ws[n] = it.reshape(P, SLOT_ELEMS)
    return ws


def _build_consts(norm_g, b_in, conv_c_w, conv_c_b, ln_c_g, ln_c_b, conv_s_w, conv_s_b, final_g):
    cst = np.zeros((P, DEPTH * LC + 10), np.float32)
    for l in range(DEPTH):
        b = l * LC
        cst[:, b:b + 60] = b_in[l].reshape(60, P).T
        cst[:, b + 60:b + 68] = norm_g[l].reshape(KD, P).T
        cst[:, b + 68:b + 192] = np.transpose(conv_c_w[l].reshape(CONF_K, NCH, P), (2, 1, 0)).reshape(P, NCH * CONF_K)
        cst[:, b + 192:b + 196] = conv_c_b[l].reshape(NCH, P).T
        cst[:, b + 196:b + 200] = ln_c_g[l].reshape(NCH, P).T
        cst[:, b + 200:b + 204] = ln_c_b[l].reshape(NCH, P).T
        cst[:, b + 204:b + 216] = np.transpose(conv_s_w[l].reshape(3, NCH, P), (2, 1, 0)).reshape(P, NCH * 3)
        cst[:, b + 216:b + 220] = conv_s_b[l].reshape(NCH, P).T
    cst[:, DEPTH * LC:DEPTH * LC + 8] = final_g.reshape(KD, P).T
    cst[:, DEPTH * LC + 8] = EPS
    return cst


_TABLES = {}


def _dft_tables():
    if "dft" in _TABLES:
        return _TABLES["dft"], _TABLES["ctab"]
    bf = ml_dtypes.bfloat16
    c = np.arange(128)
    ang = 2.0 * np.pi * ((c[:, None] * c[None, :]) % 128) / 128.0
    sc = 2.0 ** -9
    ctab = np.zeros((P, 512), np.float32)
    ctab[:, 0:128] = np.cos(ang) * sc
    ctab[:, 128:256] = -np.sin(ang) * sc
    ctab[:, 256:384] = 1.0
    ctab[:, 384:512] = np.eye(P)
    ctab = ctab.astype(bf)
    s = np.arange(512, dtype=np.int64)
    q = np.arange(S // 2, dtype=np.int64)
    dft = np.empty((8, P, 4, TW), np.float32)
    for r in range(2):
        sp = 2 * q + r
        ang = 2.0 * np.pi * ((s[:, None] * sp[None, :]) % S).astype(np.float64) / S
        Cs = np.cos(ang)
        Ss = np.sin(ang)
        Ss[0, :] = np.where(q % 2 == 0, 1.0, -1.0)
        for qt in range(2):
            for cs in range(2):
                T = Cs if cs == 0 else Ss
                blk = T[:, qt * TW:(qt + 1) * TW].reshape(4, P, TW)
                dft[(r * 2 + qt) * 2 + cs] = np.transpose(blk, (1, 0, 2))
    dft = dft.reshape(8, P, SLOT_ELEMS).astype(bf)
    _TABLES["dft"] = dft
    _TABLES["ctab"] = ctab
    return dft, ctab


_PROG = {}


def kernel(x, norm_g, w_in, b_in, conv_c_w, conv_c_b, ln_c_g, ln_c_b,
           conv_s_w, conv_s_b, w_branch, w_out, final_g):
    x = np.asarray(x, np.float32)
    w_in = np.asarray(w_in, np.float32)
    w_branch = np.asarray(w_branch, np.float32)
    w_out = np.asarray(w_out, np.float32)
    nc, w_items, _ = build_program()
    ws = _build_ws(w_items, w_in, w_branch, w_out)
    cst = _build_consts(np.asarray(norm_g, np.float32), np.asarray(b_in, np.float32),
                        np.asarray(conv_c_w, np.float32), np.asarray(conv_c_b, np.float32),
                        np.asarray(ln_c_g, np.float32), np.asarray(ln_c_b, np.float32),
                        np.asarray(conv_s_w, np.float32), np.asarray(conv_s_b, np.float32),
                        np.asarray(final_g, np.float32))
    dft, ctab = _dft_tables()
    B = x.shape[0]
    in_maps = []
    for b in range(B):
        xT = np.ascontiguousarray(np.transpose(x[b].reshape(NT, TW, KD, P), (0, 3, 2, 1))).reshape(NT, P, KD * TW)
        in_maps.append({"xT": xT, "ws": ws, "dft": dft, "cst": cst, "ctab": ctab})
    res = run_bass_kernel_spmd(nc, in_maps, core_ids=list(range(B)))
    out = np.empty((B, S, D), np.float32)
    for b in range(B):
        out[b] = np.transpose(res.results[b]["yT"].reshape(NT, P, KD, TW), (0, 3, 2, 1)).reshape(S, D)
    return out
```
